# Optimizing a Trainium2 kernel written in Bass

```python
import jax, jax.numpy as jnp
from jax import lax
import numpy as np

D_MODEL = 1024
BATCH = 2
SEQ = 8192
DEPTH = 4

N_MIXERS = 2
N_A_LAYERS = (DEPTH + N_MIXERS - 1) // N_MIXERS
N_B_LAYERS = DEPTH // N_MIXERS

HG_HEADS = 8
HG_DK = D_MODEL // HG_HEADS
HG_DV = D_MODEL // HG_HEADS
HG_CHUNK = 128
LRU_WIDTH = D_MODEL
LRU_BLOCKS = 4
LRU_BW = LRU_WIDTH // LRU_BLOCKS
CONV_W = 4
CONV_LEFT = 2
LRU_C = 8.0
PEER_HEADS = 8
PEER_NKEYS = 128
PEER_NEXPERTS = PEER_NKEYS * PEER_NKEYS
PEER_DQ = 256
PEER_TOPK = 16
PEER_BLOCK = 128
EPS = 1e-6

kernel_name = "hybrid_hgrn2_rglru_peer_encoder"


def rms_norm(x, gain):
    xf = x.astype(jnp.float32)
    inv = lax.rsqrt(jnp.mean(xf * xf, axis=-1, keepdims=True) + EPS)
    return (xf * inv).astype(x.dtype) * gain


def hgrn2_chunk_scan(q, k, v, log_f):
    B, S, H, DK = q.shape
    DV = v.shape[-1]
    C = HG_CHUNK
    nc = S // C

    def to_chunks(t):
        return t.reshape(B, nc, C, H, t.shape[-1]).transpose(1, 0, 3, 2, 4)

    qc, kc, vc, gc = to_chunks(q), to_chunks(k), to_chunks(v), to_chunks(log_f)
    lower = jnp.tril(jnp.ones((C, C), dtype=bool))[:, :, None]

    def step(state, inp):
        qi, ki, vi, gi = inp
        b = jnp.cumsum(gi.astype(jnp.float32), axis=-2)
        b_last = b[..., -1:, :]
        rel = b[..., :, None, :] - b[..., None, :, :]
        decay = jnp.exp(jnp.where(lower, rel, -jnp.inf))
        attn = jnp.sum((qi[..., :, None, :] * ki[..., None, :, :]) * decay, axis=-1)
        o_intra = jnp.einsum('bhts,bhsv->bhtv', attn, vi)
        o_inter = jnp.einsum('bhtd,bhdv->bhtv', qi * jnp.exp(b), state)
        k_dec = ki * jnp.exp(b_last - b)
        new_state = state * jnp.exp(b_last[..., 0, :])[..., None] + jnp.einsum('bhsd,bhsv->bhdv', k_dec, vi)
        return new_state, o_intra + o_inter

    state0 = jnp.zeros((B, H, DK, DV), jnp.float32)
    _, o = lax.scan(step, state0, (qc, kc, vc, gc))
    return o.transpose(1, 0, 3, 2, 4).reshape(B, S, H, DV)


def hgrn2_mixer(h, w_in, lb, norm_g, w_out):
    B, S, _ = h.shape
    proj = h @ w_in
    q, f_fw, f_bw, i_in, g = jnp.split(proj, 5, axis=-1)

    def heads(t):
        return t.reshape(B, S, HG_HEADS, -1)

    q, v, g = heads(q), heads(i_in), heads(g)

    def gates(f_logit):
        f = lb + (1.0 - lb) * jax.nn.sigmoid(f_logit.astype(jnp.float32))
        return heads(jnp.log(f)), heads(1.0 - f)

    logf_fw, k_fw = gates(f_fw)
    logf_bw, k_bw = gates(f_bw)
    rev = lambda t: t[:, ::-1]
    o_fw = hgrn2_chunk_scan(q, k_fw, v, logf_fw)
    o_bw = rev(hgrn2_chunk_scan(rev(q), rev(k_bw), rev(v), rev(logf_bw)))
    o = rms_norm(o_fw + o_bw, norm_g.reshape(HG_HEADS, HG_DV)) * jax.nn.silu(g)
    return o.reshape(B, S, D_MODEL) @ w_out


def block_diag_linear(x, w, b):
    B, S, _ = x.shape
    xr = x.reshape(B, S, LRU_BLOCKS, LRU_BW)
    return jnp.einsum('bsnk,nkj->bsnj', xr, w).reshape(B, S, LRU_WIDTH) + b


def lin_combine(left, right):
    a_l, b_l = left
    a_r, b_r = right
    return a_l * a_r, a_r * b_l + b_r


def rglru_mixer(h, w_in, conv_w, conv_b, w_a, b_a, w_x, b_x, lam, w_out):
    B, S, _ = h.shape
    proj = h @ w_in
    xb, yb = jnp.split(proj, 2, axis=-1)
    y_gate = jax.nn.gelu(yb)
    xp = jnp.pad(xb, ((0, 0), (CONV_LEFT, CONV_W - 1 - CONV_LEFT), (0, 0)))
    xc = sum(xp[:, j:j + S] * conv_w[j] for j in range(CONV_W)) + conv_b

    def direction(d, reverse):
        r = jax.nn.sigmoid(block_diag_linear(xc, w_a[d], b_a[d]).astype(jnp.float32))
        ig = jax.nn.sigmoid(block_diag_linear(xc, w_x[d], b_x[d]).astype(jnp.float32))
        log_a = -LRU_C * r * jax.nn.softplus(-lam[d].astype(jnp.float32))
        a = jnp.exp(log_a)
        mult = jnp.sqrt(-jnp.expm1(2.0 * log_a))
        _, hs = lax.associative_scan(lin_combine, (a, mult * ig * xc), axis=1, reverse=reverse)
        return hs

    hsum = direction(0, False) + direction(1, True)
    return (hsum * y_gate) @ w_out


def peer_ffn(h, w_q, sub_keys, u_tab, v_tab):
    B, S, D = h.shape
    q = (h @ w_q).reshape(B, S, PEER_HEADS, 2, PEER_DQ // 2)
    scores = jnp.einsum('bshpd,hpkd->bshpk', q, sub_keys).astype(jnp.float32)
    s1, i1 = lax.top_k(scores[..., 0, :], PEER_TOPK)
    s2, i2 = lax.top_k(scores[..., 1, :], PEER_TOPK)
    cand_s = (s1[..., :, None] + s2[..., None, :]).reshape(B, S, PEER_HEADS, PEER_TOPK * PEER_TOPK)
    cand_id = (i1[..., :, None] * PEER_NKEYS + i2[..., None, :]).reshape(B, S, PEER_HEADS, PEER_TOPK * PEER_TOPK)
    top_s, top_pos = lax.top_k(cand_s, PEER_TOPK)
    ids = jnp.take_along_axis(cand_id, top_pos, axis=-1)
    gate = jax.nn.softmax(top_s, axis=-1)

    nb = (B * S) // PEER_BLOCK
    hb = h.reshape(nb, PEER_BLOCK, D)
    idb = ids.reshape(nb, PEER_BLOCK, PEER_HEADS, PEER_TOPK)
    gb = gate.reshape(nb, PEER_BLOCK, PEER_HEADS, PEER_TOPK)

    def block(args):
        hx, idx, g = args
        u = u_tab[idx]
        act = jax.nn.gelu(jnp.einsum('thkd,td->thk', u, hx).astype(jnp.float32))
        return jnp.einsum('thk,thkd->td', g * act, v_tab[idx])

    out = lax.map(block, (hb, idb, gb))
    return out.reshape(B, S, D)


def setup_inputs(seed: int = 0) -> dict:
    key = jax.random.key(seed)
    ks = iter(jax.random.split(key, 32))
    D = D_MODEL

    def nrm(shape, scale):
        return jax.random.normal(next(ks), shape, jnp.float32) * scale

    x = nrm((BATCH, SEQ, D), 1.0)
    c = nrm((BATCH, D), 1.0)
    w_ada = nrm((DEPTH, D, 6 * D), 0.5 * D ** -0.5)
    b_ada = nrm((DEPTH, 6 * D), 0.02)
    norm_g = 1.0 + nrm((DEPTH, 2, D), 0.02)
    hg_w_in = nrm((N_A_LAYERS, D, 5 * D), D ** -0.5)
    hg_lb = nrm((N_A_LAYERS, D), 1.0)
    hg_norm_g = 1.0 + nrm((N_A_LAYERS, D), 0.02)
    hg_w_out = nrm((N_A_LAYERS, D, D), D ** -0.5)
    lru_w_in = nrm((N_B_LAYERS, D, 2 * LRU_WIDTH), D ** -0.5)
    lru_conv_w = nrm((N_B_LAYERS, CONV_W, LRU_WIDTH), 0.5)
    lru_conv_b = nrm((N_B_LAYERS, LRU_WIDTH), 0.02)
    lru_w_a = nrm((N_B_LAYERS, 2, LRU_BLOCKS, LRU_BW, LRU_BW), LRU_BW ** -0.5)
    lru_b_a = nrm((N_B_LAYERS, 2, LRU_WIDTH), 0.02)
    lru_w_x = nrm((N_B_LAYERS, 2, LRU_BLOCKS, LRU_BW, LRU_BW), LRU_BW ** -0.5)
    lru_b_x = nrm((N_B_LAYERS, 2, LRU_WIDTH), 0.02)
    a_c = jax.random.uniform(next(ks), (N_B_LAYERS, 2, LRU_WIDTH), jnp.float32, 0.9, 0.999)
    a_base = a_c ** (1.0 / LRU_C)
    lru_lam = jnp.log(a_base) - jnp.log1p(-a_base)
    lru_w_out = nrm((N_B_LAYERS, LRU_WIDTH, D), LRU_WIDTH ** -0.5)
    peer_w_q = nrm((DEPTH, D, PEER_HEADS * PEER_DQ), D ** -0.5)
    peer_keys = nrm((DEPTH, PEER_HEADS, 2, PEER_NKEYS, PEER_DQ // 2), (PEER_DQ // 2) ** -0.5)
    peer_u = nrm((DEPTH, PEER_NEXPERTS, D), D ** -0.5)
    peer_v = nrm((DEPTH, PEER_NEXPERTS, D), PEER_HEADS ** -0.5)
    final_g = 1.0 + nrm((D,), 0.02)
    return {"x": x, "c": c, "w_ada": w_ada, "b_ada": b_ada, "norm_g": norm_g,
            "hg_w_in": hg_w_in, "hg_lb": hg_lb, "hg_norm_g": hg_norm_g, "hg_w_out": hg_w_out,
            "lru_w_in": lru_w_in, "lru_conv_w": lru_conv_w, "lru_conv_b": lru_conv_b,
            "lru_w_a": lru_w_a, "lru_b_a": lru_b_a, "lru_w_x": lru_w_x, "lru_b_x": lru_b_x,
            "lru_lam": lru_lam, "lru_w_out": lru_w_out,
            "peer_w_q": peer_w_q, "peer_keys": peer_keys, "peer_u": peer_u, "peer_v": peer_v,
            "final_g": final_g}


def reference(x, c, w_ada, b_ada, norm_g, hg_w_in, hg_lb, hg_norm_g, hg_w_out,
              lru_w_in, lru_conv_w, lru_conv_b, lru_w_a, lru_b_a, lru_w_x, lru_b_x,
              lru_lam, lru_w_out, peer_w_q, peer_keys, peer_u, peer_v, final_g):
    p = jax.nn.softmax(hg_lb.astype(jnp.float32), axis=0)
    lbs = jnp.cumsum(p, axis=0) - p[0]
    cond = jax.nn.silu(c)
    for i in range(DEPTH):
        mod = (cond @ w_ada[i] + b_ada[i])[:, None, :]
        sh1, sc1, g1, sh2, sc2, g2 = jnp.split(mod, 6, axis=-1)
        h = rms_norm(x, norm_g[i, 0]) * (1.0 + sc1) + sh1
        j = i // N_MIXERS
        if i % N_MIXERS == 0:
            y = hgrn2_mixer(h, hg_w_in[j], lbs[j], hg_norm_g[j], hg_w_out[j])
        else:
            y = rglru_mixer(h, lru_w_in[j], lru_conv_w[j], lru_conv_b[j], lru_w_a[j], lru_b_a[j],
                            lru_w_x[j], lru_b_x[j], lru_lam[j], lru_w_out[j])
        x = x + g1 * y
        h = rms_norm(x, norm_g[i, 1]) * (1.0 + sc2) + sh2
        x = x + g2 * peer_ffn(h, peer_w_q[i], peer_keys[i], peer_u[i], peer_v[i])
    return rms_norm(x, final_g)
```

```python
import numpy as np
from contextlib import ExitStack
import concourse.bass as bass
import concourse.mybir as mybir
from concourse.bass_utils import run_bass_kernel_spmd

F32 = mybir.dt.float32
BF16 = mybir.dt.bfloat16
I32 = mybir.dt.int32
U32 = mybir.dt.uint32
AF = mybir.ActivationFunctionType
ALU = mybir.AluOpType
AX = mybir.AxisListType

EPOCH = 4000
DEPOCH = 250


class Sched:
    ENG = ['pe', 'act', 'dve', 'pool', 'sp']

    def __init__(self, nc, stack):
        self.nc = nc
        self.stack = stack
        self.ops = {e: [] for e in self.ENG}
        self.cnt = {e: 0 for e in self.ENG}
        self.sems = {}
        self.known = {e: {} for e in self.ENG}
        self.lastw = {}
        self.readers = {}
        self.dma_cnt = {}
        self.nsem = 0

    def _sem(self, sid):
        if sid not in self.sems:
            self.nsem += 1
            self.sems[sid] = self.stack.enter_context(self.nc.semaphore("s%d" % self.nsem))
        return self.sems[sid]

    def _deps(self, e, r, w):
        deps = {}

        def need(ev):
            if ev is None:
                return
            sid, val = ev
            if e == 'pe' and sid[0] == 'pe':
                return
            if deps.get(sid, 0) < val:
                deps[sid] = val
        for k in r:
            need(self.lastw.get(k))
        for k in w:
            need(self.lastw.get(k))
            for sid, val in self.readers.get(k, {}).items():
                need((sid, val))
        waits = []
        kn = self.known[e]
        for sid, val in deps.items():
            if kn.get(sid, 0) < val:
                kn[sid] = val
                waits.append((self._sem(sid), val))
        return waits

    def _record(self, ev, r, w):
        sid, val = ev
        for k in r:
            d = self.readers.setdefault(k, {})
            if d.get(sid, 0) < val:
                d[sid] = val
        for k in w:
            self.lastw[k] = ev
            self.readers[k] = {}

    def op(self, e, fn, r=(), w=()):
        waits = self._deps(e, r, w)
        n = self.cnt[e]
        self.cnt[e] = n + 1
        sid = (e, n // EPOCH)
        val = n % EPOCH + 1
        semh = self._sem(sid)

        def run(eng):
            for s, v in waits:
                eng.wait_ge(s, v)
            fn(eng).then_inc(semh, 1)
        self.ops[e].append(run)
        self._record((sid, val), r, w)

    def dma(self, q, fn, semkey, r=(), w=()):
        waits = self._deps(q, r, w)
        c = self.dma_cnt.get(semkey, 0)
        self.dma_cnt[semkey] = c + 1
        sid = ('dma', semkey, c // DEPOCH)
        val = 16 * (c % DEPOCH + 1)
        semh = self._sem(sid)

        def run(eng):
            for s, v in waits:
                eng.wait_ge(s, v)
            fn(eng).then_inc(semh, 16)
        self.ops[q].append(run)
        self._record((sid, val), r, w)

    def fence(self, e, r=(), w=()):
        waits = self._deps(e, r, w)

        def run(eng):
            for s, v in waits:
                eng.wait_ge(s, v)
        self.ops[e].append(run)

    def emit(self):
        nc = self.nc
        with nc.Block() as block:
            @block.tensor
            def _(eng):
                for f in self.ops['pe']:
                    f(eng)

            @block.scalar
            def _(eng):
                for f in self.ops['act']:
                    f(eng)

            @block.vector
            def _(eng):
                for f in self.ops['dve']:
                    f(eng)

            @block.gpsimd
            def _(eng):
                for f in self.ops['pool']:
                    f(eng)

            @block.sync
            def _(eng):
                for f in self.ops['sp']:
                    f(eng)


NT = 16
STAGE = 4
GS = 4
NEG = -1.0e30


class Prog:
    def __init__(self):
        self.nc = bass.Bass("TRN2", target_bir_lowering=False)
        self.st = ExitStack()
        self.S = Sched(self.nc, self.st)

    def din(self, name, shape, dt=F32):
        return self.nc.dram_tensor(name, list(shape), dt, kind="ExternalInput").ap()

    def dout(self, name, shape, dt=F32):
        return self.nc.dram_tensor(name, list(shape), dt, kind="ExternalOutput").ap()

    def sb(self, name, shape, dt=F32):
        return self.st.enter_context(self.nc.sbuf_tensor("s_" + name, list(shape), dt))

    def ps(self, name, shape=(128, 512), dt=F32):
        return self.st.enter_context(self.nc.psum_tensor("p_" + name, list(shape), dt))


def make_consts(P):
    S = P.S
    c = {}
    c['ident'] = P.sb("ident", [128, 128])
    c['identb'] = P.sb("identb", [128, 128], BF16)
    c['ones'] = P.sb("ones", [128, 128])
    c['io'] = P.sb("io", [128, 128])
    c['iota16'] = P.sb("iota16", [128, 16])
    c['lo17'] = P.sb("lo17", [128, 17])
    c['eps'] = P.sb("epst", [128, 1])
    S.op('pool', lambda e: e.iota(c['io'][:], [[1, 128]], base=0, channel_multiplier=-1,
                                  allow_small_or_imprecise_dtypes=True), w=['c_io'])
    S.op('dve', lambda e: e.tensor_single_scalar(c['ident'][:], c['io'][:], 0.0, ALU.is_equal), r=['c_io'], w=['c_ident'])
    S.op('dve', lambda e: e.tensor_copy(out=c['identb'][:], in_=c['ident'][:]), r=['c_ident'], w=['c_identb'])
    S.op('dve', lambda e: e.memset(c['ones'][:], 1.0), w=['c_ones'])
    S.op('dve', lambda e: e.memset(c['eps'][:], 1e-6), w=['c_eps'])
    S.op('pool', lambda e: e.iota(c['iota16'][:], [[1, 16]], base=0, channel_multiplier=0,
                                  allow_small_or_imprecise_dtypes=True), w=['c_iota16'])
    S.op('pool', lambda e: e.iota(c['lo17'][:], [[16, 17]], base=0, channel_multiplier=0,
                                  allow_small_or_imprecise_dtypes=True), w=['c_lo17'])
    return c


def mod_vectors(P, c, cvec_d, wada_d, bada_d, nblk, wbig, psA, name):
    S = P.S
    cv = P.sb(name + "_cv", [128, 8])
    cond = P.sb(name + "_cond", [128, 8])
    bada = P.sb(name + "_bada", [128, nblk * 8])
    mod = P.sb(name + "_mod", [128, nblk, 8])
    S.dma('sp', lambda e: e.dma_start(out=cv[:], in_=cvec_d), name + 'cv', w=[name + 'cv'])
    S.dma('sp', lambda e: e.dma_start(out=bada[:], in_=bada_d), name + 'bada', w=[name + 'bada'])
    S.op('act', lambda e: e.activation(out=cond[:], in_=cv[:], func=AF.Silu), r=[name + 'cv'], w=[name + 'cond'])
    for b in range(nblk):
        for hf in range(2):
            S.dma('sp', lambda e, b=b, hf=hf: e.dma_start(
                out=wbig[:, hf * 4:(hf + 1) * 4, :],
                in_=wada_d[hf * 512:(hf + 1) * 512, b * 1024:(b + 1) * 1024].rearrange("(kc p) n -> p kc n", p=128)),
                'wbig%d' % hf, w=[('wbig', hf)])
        for j in range(8):
            for kc in range(8):
                S.op('pe', lambda e, j=j, kc=kc: e.matmul(psA[:, j:j + 1], lhsT=wbig[:, kc, j * 128:(j + 1) * 128],
                                                            rhs=cond[:, kc:kc + 1], start=(kc == 0), stop=(kc == 7)),
                     r=[('wbig', kc // 4), name + 'cond'], w=['psA'])
        S.op('dve', lambda e, b=b: e.tensor_tensor(out=mod[:, b, :], in0=psA[:, 0:8], in1=bada[:, b * 8:(b + 1) * 8], op=ALU.add),
             r=['psA', name + 'bada'], w=[(name + 'mod', b)])
    return mod


def bcast_tile(P, c, vec_ap, key_r, out_tile, key_w, dtmp, psB):
    S = P.S
    for hf in range(2):
        for jj in range(4):
            j = hf * 4 + jj
            S.op('dve', lambda e, j=j: e.tensor_scalar(out=dtmp[:], in0=c['ident'][:], scalar1=vec_ap[:, j:j + 1], scalar2=None,
                                                        op0=ALU.mult), r=['c_ident'] + key_r, w=['dtmp'])
            S.op('pe', lambda e, jj=jj: e.matmul(psB[:, jj * 128:(jj + 1) * 128], lhsT=c['ones'][:], rhs=dtmp[:],
                                                  start=True, stop=True), r=['c_ones', 'dtmp'], w=['psB'])
        S.op('dve', lambda e, hf=hf: e.tensor_copy(out=out_tile[:, hf * 512:(hf + 1) * 512], in_=psB[:, 0:512]),
             r=['psB'], w=[(key_w, hf)])


def build_peer(final, debug=False):
    P = Prog()
    nc, S = P.nc, P.S
    x_d = P.din("x", [NT * 128, 1024])
    cvec_d = P.din("cvec", [128, 8])
    wada_d = P.din("wada", [1024, 3072])
    bada_d = P.din("bada", [128, 24])
    ng_d = P.din("ng", [128, 8])
    wq_d = P.din("wq", [1024, 2048])
    keys_d = P.din("keys", [16, 128, 128])
    u_d = P.din("utab", [16384, 1024])
    v_d = P.din("vtab", [16384, 1024])
    fg_d = P.din("fg", [128, 8])
    y_d = P.dout("y", [NT * 128, 1024])
    if debug:
        dbg_ids = P.dout("dbg_ids", [NT * 128, 128], U32)
        dbg_gt = P.dout("dbg_gt", [NT * 128, 128])

    c = make_consts(P)
    psA = P.ps("psA0")
    psA1 = P.ps("psA1")
    psQ = [P.ps("psQ0"), P.ps("psQ1")]
    psS = [P.ps("psS0"), P.ps("psS1")]
    psO = [P.ps("psO0"), P.ps("psO1")]

    Uall = P.sb("Uall", [128, 2 * GS, 1024])
    assert 2 * GS == 8
    Ub = [Uall[:, i * GS:(i + 1) * GS, :] for i in range(2)]
    wbig = Uall
    mod = mod_vectors(P, c, cvec_d, wada_d, bada_d, 3, wbig, psA, "m")
    ng = P.sb("ng", [128, 8])
    S.dma('sp', lambda e: e.dma_start(out=ng[:], in_=ng_d), 'ng', w=['ng'])
    Sc = P.sb("Sc", [128, 8])
    S.op('dve', lambda e: e.scalar_tensor_tensor(out=Sc[:], in0=mod[:, 1, :], scalar=1.0, in1=ng[:], op0=ALU.add, op1=ALU.mult),
         r=[('mmod', 1), 'ng'], w=['Sc'])
    dtmp = P.sb("dtmp", [128, 128])
    ScB = P.sb("ScB", [128, 1024])
    ShB = P.sb("ShB", [128, 1024])
    G2B = P.sb("G2B", [128, 1024])
    bcast_tile(P, c, Sc, ['Sc'], ScB, 'ScB', dtmp, psA1)
    bcast_tile(P, c, mod[:, 0, :], [('mmod', 0)], ShB, 'ShB', dtmp, psA1)
    bcast_tile(P, c, mod[:, 2, :], [('mmod', 2)], G2B, 'G2B', dtmp, psA1)
    if final:
        fg = P.sb("fg", [128, 8])
        FGB = P.sb("FGB", [128, 1024])
        S.dma('sp', lambda e: e.dma_start(out=fg[:], in_=fg_d), 'fg', w=['fg'])
        bcast_tile(P, c, fg, ['fg'], FGB, 'FGB', dtmp, psA1)
    Wq = P.sb("Wq", [128, 8, 2048], BF16)
    for kc in range(8):
        S.dma('pool', lambda e, kc=kc: e.dma_start(out=Wq[:, kc, :], in_=wq_d[kc * 128:(kc + 1) * 128, :]),
              'Wq%d' % kc, w=[('Wq', kc)])
    keysT = P.sb("keysT", [128, 16, 128], BF16)
    kraw = wbig
    for hp in range(16):
        S.dma('sp', lambda e, hp=hp: e.dma_start(out=kraw[:, hp // 8, (hp % 8) * 128:(hp % 8 + 1) * 128], in_=keys_d[hp]),
              'kraw', r=[('mmod', 0), ('mmod', 1), ('mmod', 2)], w=[('wbig', 0), ('wbig', 1)])
    for hp in range(16):
        S.op('pe', lambda e, hp=hp: e.transpose(out=psA1[:, (hp % 4) * 128:(hp % 4 + 1) * 128],
                                                 in_=kraw[:, hp // 8, (hp % 8) * 128:(hp % 8 + 1) * 128], identity=c['ident'][:]),
             r=[('wbig', 0), ('wbig', 1), 'c_ident'], w=['psB'])
        if hp % 4 == 3:
            S.op('act', lambda e, hp=hp: e.copy(out=keysT[:, hp - 3:hp + 1, :], in_=psA1[:, 0:512]), r=['psB'], w=['keysT'])

    xt = [P.sb("xt%d" % i, [128, 1024]) for i in range(2)]
    junk = P.sb("junk", [128, 1024], BF16)
    ss = P.sb("ss", [128, 4])
    xn = P.sb("xn", [128, 1024])
    h2tok = [P.sb("h2tok%d" % i, [128, 1024]) for i in range(2)]
    h2T = P.sb("h2T", [128, 8, 128], BF16)
    qT = P.sb("qT", [128, 16, 128], BF16)
    scs = P.sb("scs", [128, 2048])
    wk = P.sb("wk", [128, 256])
    mx = P.sb("mx", [128, 8, 2, 16])
    ix = P.sb("ix", [128, 8, 2, 16], U32)
    ixf = P.sb("ixf", [128, 8, 2, 16])
    cand = P.sb("cand", [128, 8, 256])
    tv = P.sb("tv", [128, 8, 16])
    tp = P.sb("tp", [128, 8, 16], U32)
    tpf = P.sb("tpf", [128, 8, 16])
    ge = P.sb("ge", [128, 8, 16, 17])
    oh = P.sb("oh", [128, 8, 16, 16])
    ak = P.sb("ak", [128, 8, 16])
    bk = P.sb("bk", [128, 8, 16])
    i1s = P.sb("i1s", [128, 8, 16])
    i2s = P.sb("i2s", [128, 8, 16])
    idf = P.sb("idf", [128, 8, 16])
    ids = [P.sb("ids%d" % i, [128, 128], U32) for i in range(2)]
    gt = [P.sb("gt%d" % i, [128, 8, 16]) for i in range(2)]
    gsum = P.sb("gsum", [128, 8])
    dots = P.sb("dots", [128, 128])
    wgt = P.sb("wgt", [128, 128])
    Vall = P.sb("Vall", [128, 2 * GS, 1024], BF16)
    Vb = [Vall[:, i * GS:(i + 1) * GS, :] for i in range(2)]
    Dg = [P.sb("Dg%d" % i, [128, 128], BF16) for i in range(4)]
    sjunk = P.sb("sjunk", [128, 1024], BF16)
    S.fence('pool', w=[('wbig', 0), ('wbig', 1)])

    ngroups = 128 // GS
    state = dict(gcount=0, dcount=0)

    def front(tt):
        xb = tt % 2
        X = xt[xb]
        kx = ('xt', xb)
        H = h2tok[xb]
        kh = ('h2tok', xb)
        GT = gt[xb]
        kg = ('gt', xb)
        IDS = ids[xb]
        kid = ('ids', xb)
        S.dma('sp', lambda e: e.dma_start(out=X[:], in_=x_d[tt * 128:(tt + 1) * 128, :]), 'xt%d' % xb, w=[kx])
        S.op('act', lambda e: e.activation(out=junk[:], in_=X[:], func=AF.Square, accum_out=ss[:, 0:1]),
             r=[kx], w=['junk', 'ss0'])
        S.op('act', lambda e: e.activation(out=ss[:, 1:2], in_=ss[:, 0:1], func=AF.Ln, scale=1.0 / 1024, bias=c['eps'][:]),
             r=['ss0', 'c_eps'], w=['ss1'])
        S.op('act', lambda e: e.activation(out=ss[:, 2:3], in_=ss[:, 1:2], func=AF.Exp, scale=-0.5), r=['ss1'], w=['ss2'])
        S.op('dve', lambda e: e.tensor_scalar(out=xn[:], in0=X[:], scalar1=ss[:, 2:3], scalar2=None, op0=ALU.mult),
             r=[kx, 'ss2'], w=['xn'])
        S.op('pool', lambda e: e.tensor_tensor(out=H[:], in0=xn[:], in1=ScB[:], op=ALU.mult),
             r=['xn', ('ScB', 0), ('ScB', 1)], w=[kh])
        S.op('pool', lambda e: e.tensor_tensor(out=H[:], in0=H[:], in1=ShB[:], op=ALU.add),
             r=[kh, ('ShB', 0), ('ShB', 1)], w=[kh])
        for hf in range(2):
            pst = psA if hf == 0 else psA1
            kp = 'psA' if hf == 0 else 'psB'
            for jj in range(4):
                fc = hf * 4 + jj
                S.op('pe', lambda e, fc=fc, jj=jj, pst=pst: e.transpose(out=pst[:, jj * 128:(jj + 1) * 128],
                                                                       in_=xn[:, fc * 128:(fc + 1) * 128], identity=c['ident'][:]),
                     r=['xn', 'c_ident'], w=[kp])
            for jj in range(4):
                fc = hf * 4 + jj
                S.op('act', lambda e, fc=fc, jj=jj, pst=pst: e.activation(out=h2T[:, fc, :], in_=pst[:, jj * 128:(jj + 1) * 128],
                                                                         func=AF.Identity, scale=Sc[:, fc:fc + 1], bias=mod[:, 0, fc:fc + 1]),
                     r=[kp, 'Sc', ('mmod', 0)], w=[('h2T', fc)])
        for g4 in range(4):
            pq = psQ[g4 % 2]
            kq = ('psQ', g4 % 2)
            for q in range(4):
                hp = g4 * 4 + q
                for kc in range(8):
                    S.op('pe', lambda e, hp=hp, q=q, kc=kc, pq=pq: e.matmul(pq[:, q * 128:(q + 1) * 128],
                                                                           lhsT=Wq[:, kc, hp * 128:(hp + 1) * 128], rhs=h2T[:, kc, :],
                                                                           start=(kc == 0), stop=(kc == 7)),
                         r=[('Wq', kc), ('h2T', kc)], w=[kq])
            S.op('dve', lambda e, g4=g4, pq=pq: e.tensor_copy(out=qT[:, g4 * 4:(g4 + 1) * 4, :], in_=pq[:, 0:512]),
                 r=[kq], w=[('qT', g4)])
            pss = psS[g4 % 2]
            ks = ('psS', g4 % 2)
            for q in range(4):
                hp = g4 * 4 + q
                S.op('pe', lambda e, hp=hp, q=q, pss=pss: e.matmul(pss[:, q * 128:(q + 1) * 128], lhsT=qT[:, hp, :], rhs=keysT[:, hp, :],
                                                                   start=True, stop=True),
                     r=[('qT', g4), 'keysT'], w=[ks])
            S.op('act', lambda e, g4=g4, pss=pss: e.copy(out=scs[:, g4 * 512:(g4 + 1) * 512], in_=pss[:, 0:512]),
                 r=[ks], w=[('scs', g4)])
            for q in range(4):
                hp = g4 * 4 + q
                h, p = hp // 2, hp % 2
                src = scs[:, hp * 128:(hp + 1) * 128]
                km = ('mx', hp)
                S.op('dve', lambda e, h=h, p=p, src=src: e.max(out=mx[:, h, p, 0:8], in_=src), r=[('scs', g4)], w=[km])
                S.op('dve', lambda e, h=h, p=p, src=src: e.max_index(out=ix[:, h, p, 0:8], in_max=mx[:, h, p, 0:8], in_values=src),
                     r=[('scs', g4), km], w=[('ix', hp)])
                S.op('dve', lambda e, h=h, p=p, src=src: e.match_replace(out=wk[:, 0:128], in_to_replace=mx[:, h, p, 0:8],
                                                                         in_values=src, imm_value=NEG),
                     r=[('scs', g4), km], w=['wk'])
                S.op('dve', lambda e, h=h, p=p: e.max(out=mx[:, h, p, 8:16], in_=wk[:, 0:128]), r=['wk'], w=[km])
                S.op('dve', lambda e, h=h, p=p: e.max_index(out=ix[:, h, p, 8:16], in_max=mx[:, h, p, 8:16], in_values=wk[:, 0:128]),
                     r=['wk', km], w=[('ix', hp)])
        allmx = [('mx', hp) for hp in range(16)]
        allix = [('ix', hp) for hp in range(16)]
        S.op('dve', lambda e: e.tensor_tensor(out=cand[:].rearrange("p h (a b) -> p h a b", a=16),
                                              in0=mx[:, :, 0, :].unsqueeze(3).to_broadcast([128, 8, 16, 16]),
                                              in1=mx[:, :, 1, :].unsqueeze(2).to_broadcast([128, 8, 16, 16]), op=ALU.add),
             r=allmx, w=['cand'])
        for h in range(8):
            S.op('dve', lambda e, h=h: e.max(out=tv[:, h, 0:8], in_=cand[:, h, :]), r=['cand'], w=[('tv', h)])
            S.op('dve', lambda e, h=h: e.max_index(out=tp[:, h, 0:8], in_max=tv[:, h, 0:8], in_values=cand[:, h, :]),
                 r=['cand', ('tv', h)], w=[('tp', h)])
            S.op('dve', lambda e, h=h: e.match_replace(out=wk[:], in_to_replace=tv[:, h, 0:8], in_values=cand[:, h, :], imm_value=NEG),
                 r=['cand', ('tv', h)], w=['wk'])
            S.op('dve', lambda e, h=h: e.max(out=tv[:, h, 8:16], in_=wk[:]), r=['wk'], w=[('tv', h)])
            S.op('dve', lambda e, h=h: e.max_index(out=tp[:, h, 8:16], in_max=tv[:, h, 8:16], in_values=wk[:]),
                 r=['wk', ('tv', h)], w=[('tp', h)])
        alltv = [('tv', h) for h in range(8)]
        alltp = [('tp', h) for h in range(8)]
        S.op('dve', lambda e: e.tensor_tensor(out=GT[:], in0=tv[:], in1=tv[:, :, 0:1].to_broadcast([128, 8, 16]), op=ALU.subtract),
             r=alltv, w=[kg])
        S.op('act', lambda e: e.activation(out=GT[:], in_=GT[:], func=AF.Exp), r=[kg], w=[kg])
        S.op('dve', lambda e: e.tensor_reduce(out=gsum[:], in_=GT[:], axis=AX.X, op=ALU.add), r=[kg], w=['gsum'])
        S.op('dve', lambda e: e.reciprocal(out=gsum[:], in_=gsum[:]), r=['gsum'], w=['gsum'])
        S.op('dve', lambda e: e.tensor_tensor(out=GT[:], in0=GT[:], in1=gsum[:].unsqueeze(2).to_broadcast([128, 8, 16]), op=ALU.mult),
             r=[kg, 'gsum'], w=[kg])
        S.op('dve', lambda e: e.tensor_copy(out=tpf[:], in_=tp[:]), r=alltp, w=['tpf'])
        S.op('dve', lambda e: e.tensor_copy(out=ixf[:], in_=ix[:]), r=allix, w=['ixf'])
        S.op('dve', lambda e: e.tensor_tensor(out=ge[:], in0=tpf[:].unsqueeze(3).to_broadcast([128, 8, 16, 17]),
                                              in1=c['lo17'][:].unsqueeze(1).unsqueeze(1).to_broadcast([128, 8, 16, 17]), op=ALU.is_ge),
             r=['tpf', 'c_lo17'], w=['ge'])
        S.op('dve', lambda e: e.tensor_tensor(out=oh[:], in0=ge[:, :, :, 0:16], in1=ge[:, :, :, 1:17], op=ALU.subtract),
             r=['ge'], w=['oh'])
        S.op('dve', lambda e: e.tensor_reduce(out=ak[:], in_=ge[:, :, :, 1:17], axis=AX.X, op=ALU.add), r=['ge'], w=['ak'])
        S.op('dve', lambda e: e.tensor_tensor(out=oh[:], in0=oh[:], in1=ixf[:, :, 0, :].unsqueeze(2).to_broadcast([128, 8, 16, 16]),
                                              op=ALU.mult), r=['oh', 'ixf'], w=['oh'])
        S.op('dve', lambda e: e.tensor_reduce(out=i1s[:], in_=oh[:], axis=AX.X, op=ALU.add), r=['oh'], w=['i1s'])
        S.op('dve', lambda e: e.scalar_tensor_tensor(out=bk[:], in0=ak[:], scalar=-16.0, in1=tpf[:], op0=ALU.mult, op1=ALU.add),
             r=['ak', 'tpf'], w=['bk'])
        S.op('dve', lambda e: e.tensor_tensor(out=oh[:], in0=bk[:].unsqueeze(3).to_broadcast([128, 8, 16, 16]),
                                              in1=c['iota16'][:].unsqueeze(1).unsqueeze(1).to_broadcast([128, 8, 16, 16]), op=ALU.is_equal),
             r=['bk', 'c_iota16', 'i1s'], w=['oh'])
        S.op('dve', lambda e: e.tensor_tensor(out=oh[:], in0=oh[:], in1=ixf[:, :, 1, :].unsqueeze(2).to_broadcast([128, 8, 16, 16]),
                                              op=ALU.mult), r=['oh', 'ixf'], w=['oh'])
        S.op('dve', lambda e: e.tensor_reduce(out=i2s[:], in_=oh[:], axis=AX.X, op=ALU.add), r=['oh'], w=['i2s'])
        S.op('dve', lambda e: e.scalar_tensor_tensor(out=idf[:], in0=i1s[:], scalar=128.0, in1=i2s[:], op0=ALU.mult, op1=ALU.add),
             r=['i1s', 'i2s'], w=['idf'])
        S.op('dve', lambda e: e.tensor_copy(out=IDS[:], in_=idf[:].rearrange("p h k -> p (h k)")), r=['idf'], w=[kid])

    def back(tt):
        xb = tt % 2
        X = xt[xb]
        kx = ('xt', xb)
        H = h2tok[xb]
        kh = ('h2tok', xb)
        GT = gt[xb]
        kg = ('gt', xb)
        IDS = ids[xb]
        kid = ('ids', xb)
        if debug:
            S.dma('sp', lambda e: e.dma_start(out=dbg_ids[tt * 128:(tt + 1) * 128, :], in_=IDS[:]), 'dbg%d' % xb, r=[kid], w=[('y', tt, 1)])
            S.dma('sp', lambda e: e.dma_start(out=dbg_gt[tt * 128:(tt + 1) * 128, :], in_=GT[:].rearrange("p h k -> p (h k)")), 'dbg%d' % xb, r=[kg], w=[('y', tt, 2)])
            S.dma('sp', lambda e: e.dma_start(out=y_d[tt * 128:(tt + 1) * 128, :], in_=H[:]), 'y%d' % xb, r=[kh], w=[('y', tt)])
            return
        for g in range(ngroups):
            b = state['gcount'] % 2
            state['gcount'] += 1
            for s in range(GS):
                sl = g * GS + s
                S.dma('pool', lambda e, b=b, s=s, sl=sl: e.indirect_dma_start(
                    out=Ub[b][:, s, :], out_offset=None, in_=u_d,
                    in_offset=bass.IndirectOffsetOnAxis(ap=IDS[:, sl:sl + 1], axis=0)),
                    'Ub%d' % b, r=[kid], w=[('Ub', b, s)])
                S.dma('pool', lambda e, b=b, s=s, sl=sl: e.indirect_dma_start(
                    out=Vb[b][:, s, :], out_offset=None, in_=v_d,
                    in_offset=bass.IndirectOffsetOnAxis(ap=IDS[:, sl:sl + 1], axis=0)),
                    'Vb%d' % b, r=[kid], w=[('Vb', b, s)])
            allU = [('Ub', b, s) for s in range(GS)]
            allV = [('Vb', b, s) for s in range(GS)]
            if STAGE < 2:
                S.op('dve', lambda e, b=b, g=g: e.tensor_copy(out=dots[:, g * GS:(g + 1) * GS], in_=Ub[b][:, :, 0]), r=allU, w=[('dots', g)])
                S.op('dve', lambda e, b=b, g=g: e.tensor_copy(out=wgt[:, g * GS:(g + 1) * GS], in_=Vb[b][:, :, 0]), r=allV, w=[('wgt', g)])
                continue
            for s in range(GS):
                sl = g * GS + s
                S.op('dve', lambda e, b=b, s=s, sl=sl: e.scalar_tensor_tensor(
                    out=sjunk[:], in0=Ub[b][:, s, :], scalar=1.0, in1=H[:], op0=ALU.mult, op1=ALU.mult,
                    accum_out=dots[:, sl:sl + 1]), r=allU + [kh], w=['sjunk', ('dots', g)])
            gsl = slice(g * GS, (g + 1) * GS)
            if STAGE < 3:
                S.op('dve', lambda e, b=b, g=g: e.tensor_copy(out=wgt[:, g * GS:(g + 1) * GS], in_=Vb[b][:, :, 0]), r=allV, w=[('wgt', g)])
                continue
            S.op('act', lambda e, gsl=gsl: e.activation(out=wgt[:, gsl], in_=dots[:, gsl], func=AF.Gelu_apprx_tanh),
                 r=[('dots', g)], w=[('wgt', g)])
            S.op('dve', lambda e, gsl=gsl: e.tensor_tensor(out=wgt[:, gsl], in0=wgt[:, gsl],
                                                           in1=GT[:].rearrange("p h k -> p (h k)")[:, gsl], op=ALU.mult),
                 r=[('wgt', g), kg], w=[('wgt', g)])
            for s in range(GS):
                sl = g * GS + s
                d = state['dcount'] % 4
                state['dcount'] += 1
                S.op('act', lambda e, d=d, sl=sl: e.activation(out=Dg[d][:], in_=c['identb'][:], func=AF.Copy, scale=wgt[:, sl:sl + 1]),
                     r=['c_identb', ('wgt', g)], w=[('Dg', d)])
                for hf in range(2 if STAGE >= 4 else 0):
                    S.op('pe', lambda e, d=d, b=b, s=s, hf=hf, sl=sl: e.matmul(psO[hf][:, 0:512], lhsT=Dg[d][:],
                                                                             rhs=Vb[b][:, s, hf * 512:(hf + 1) * 512],
                                                                             start=(sl == 0), stop=(sl == 127)),
                         r=[('Dg', d)] + allV, w=[('psO', hf)])
        if STAGE < 4:
            S.op('dve', lambda e: e.tensor_copy(out=X[:, 0:128], in_=dots[:]), r=[('dots', g) for g in range(ngroups)] + [kx], w=[kx])
            S.op('dve', lambda e: e.tensor_copy(out=X[:, 128:256], in_=wgt[:]), r=[('wgt', g) for g in range(ngroups)] + [kx], w=[kx])
            if STAGE == 3:
                for d in range(4):
                    S.op('dve', lambda e, d=d: e.tensor_copy(out=X[:, 256 + d * 128:384 + d * 128], in_=Dg[d][:]), r=[('Dg', d), kx], w=[kx])
        for hf in range(2 if STAGE >= 4 else 0):
            fs = slice(hf * 512, (hf + 1) * 512)
            S.op('dve', lambda e, hf=hf, fs=fs: e.tensor_tensor(out=xn_o[:, fs], in0=psO[hf][:, 0:512], in1=G2B[:, fs], op=ALU.mult),
                 r=[('psO', hf), ('G2B', hf)], w=['xn_o'])
        if STAGE >= 4:
            S.op('dve', lambda e: e.tensor_tensor(out=X[:], in0=xn_o[:], in1=X[:], op=ALU.add), r=['xn_o', kx], w=[kx])
        if final:
            S.op('act', lambda e: e.activation(out=junk[:], in_=X[:], func=AF.Square, accum_out=ss[:, 0:1]),
                 r=[kx], w=['junk', 'ss0'])
            S.op('act', lambda e: e.activation(out=ss[:, 1:2], in_=ss[:, 0:1], func=AF.Ln, scale=1.0 / 1024, bias=c['eps'][:]),
                 r=['ss0', 'c_eps'], w=['ss1'])
            S.op('act', lambda e: e.activation(out=ss[:, 3:4], in_=ss[:, 1:2], func=AF.Exp, scale=-0.5), r=['ss1'], w=['ss3'])
            S.op('dve', lambda e: e.scalar_tensor_tensor(out=X[:], in0=X[:], scalar=ss[:, 3:4], in1=FGB[:],
                                                         op0=ALU.mult, op1=ALU.mult),
                 r=[kx, 'ss3', ('FGB', 0), ('FGB', 1)], w=[kx])
        S.dma('sp', lambda e: e.dma_start(out=y_d[tt * 128:(tt + 1) * 128, :], in_=X[:]), 'y%d' % xb, r=[kx], w=[('y', tt)])

    xn_o = P.sb("xn_o", [128, 1024])
    front(0)
    for tt in range(NT):
        if tt + 1 < NT:
            front(tt + 1)
        back(tt)
    S.fence('sp', r=[('y', tt) for tt in range(NT)] + ([('y', tt, 1) for tt in range(NT)] + [('y', tt, 2) for tt in range(NT)] if debug else []))
    S.emit()
    P.st.close()
    return nc


NTOK = 2048


def mixer_front(P, c, d, psA, psA1, wbig, need_halo=False):
    S = P.S
    mod = mod_vectors(P, c, d['cvec'], d['wada'], d['bada'], 3, wbig, psA, "m")
    ng = P.sb("ng", [128, 8])
    S.dma('sp', lambda e: e.dma_start(out=ng[:], in_=d['ng']), 'ng', w=['ng'])
    Sc = P.sb("Sc", [128, 8])
    S.op('dve', lambda e: e.scalar_tensor_tensor(out=Sc[:], in0=mod[:, 1, :], scalar=1.0, in1=ng[:], op0=ALU.add, op1=ALU.mult),
         r=[('mmod', 1), 'ng'], w=['Sc'])
    dtmp = P.sb("dtmp", [128, 128])
    G1B = P.sb("G1B", [128, 1024])
    bcast_tile(P, c, mod[:, 2, :], [('mmod', 2)], G1B, 'G1B', dtmp, psA1)
    hT = P.sb("hT", [128, 8, NTOK], BF16)
    xt = [P.sb("xt%d" % i, [128, 1024]) for i in range(2)]
    junk = P.sb("junk", [128, 1024], BF16)
    ss = P.sb("ss", [128, 4])
    xn = P.sb("xn", [128, 1024])

    def norm_tile(X, kx, npart, dst_fn, kdst_fn):
        S.op('act', lambda e: e.activation(out=junk[0:npart, :], in_=X[0:npart, :], func=AF.Square, accum_out=ss[0:npart, 0:1]),
             r=[kx], w=['junk', 'ss0'])
        S.op('act', lambda e: e.activation(out=ss[0:npart, 1:2], in_=ss[0:npart, 0:1], func=AF.Ln, scale=1.0 / 1024,
                                           bias=c['eps'][0:npart, :]), r=['ss0', 'c_eps'], w=['ss1'])
        S.op('act', lambda e: e.activation(out=ss[0:npart, 2:3], in_=ss[0:npart, 1:2], func=AF.Exp, scale=-0.5), r=['ss1'], w=['ss2'])
        S.op('dve', lambda e: e.tensor_scalar(out=xn[0:npart, :], in0=X[0:npart, :], scalar1=ss[0:npart, 2:3], scalar2=None, op0=ALU.mult),
             r=[kx, 'ss2'], w=['xn'])
        for hf in range(2):
            pst = psA if hf == 0 else psA1
            kp = 'psA' if hf == 0 else 'psB'
            for jj in range(4):
                fc = hf * 4 + jj
                S.op('pe', lambda e, fc=fc, jj=jj, pst=pst: e.transpose(out=pst[:, jj * 128:jj * 128 + npart],
                                                                       in_=xn[0:npart, fc * 128:(fc + 1) * 128],
                                                                       identity=c['ident'][0:npart, 0:npart]),
                     r=['xn', 'c_ident'], w=[kp])
            for jj in range(4):
                fc = hf * 4 + jj
                S.op('act', lambda e, fc=fc, jj=jj, pst=pst: e.activation(out=dst_fn(fc), in_=pst[:, jj * 128:jj * 128 + npart],
                                                                         func=AF.Identity, scale=Sc[:, fc:fc + 1], bias=mod[:, 0, fc:fc + 1]),
                     r=[kp, 'Sc', ('mmod', 0)], w=[kdst_fn(fc)])

    for tt in range(NTOK // 128):
        xb = tt % 2
        X = xt[xb]
        kx = ('xt', xb)
        S.dma('sp', lambda e, tt=tt, X=X: e.dma_start(out=X[:], in_=d['x'][tt * 128:(tt + 1) * 128, :]), 'xt%d' % xb, w=[kx])
        norm_tile(X, kx, 128, lambda fc, tt=tt: hT[:, fc, tt * 128:(tt + 1) * 128], lambda fc, tt=tt: ('hT', fc, tt // 4))
    hTh = None
    if need_halo:
        hTh = P.sb("hTh", [128, 8, 4], BF16)
        X = xt[0]
        kx = ('xt', 0)
        S.dma('sp', lambda e: e.dma_start(out=X[0:3, :], in_=d['xhalo']), 'xt0', w=[kx])
        norm_tile(X, kx, 3, lambda fc: hTh[:, fc, 0:3], lambda fc: 'hTh')
    return dict(mod=mod, G1B=G1B, hT=hT, hTh=hTh, xt=xt, dtmp=dtmp)


def hT_keys(tb):
    return [('hT', fc, tb) for fc in range(8)]


def mixer_back(P, c, d, fr, zT, Wo, psY, kY):
    S = P.S
    xt = fr['xt']
    G1B = fr['G1B']
    yt = P.sb("yt", [128, 1024])
    for tt in range(NTOK // 128):
        xb = tt % 2
        X = xt[xb]
        kx = ('xt', xb)
        S.dma('sp', lambda e, tt=tt, X=X: e.dma_start(out=X[:], in_=d['x'][tt * 128:(tt + 1) * 128, :]), 'xt%d' % xb, w=[kx])
        for hf in range(2):
            for j in range(8):
                S.op('pe', lambda e, hf=hf, j=j, tt=tt: e.matmul(psY[hf][:, 0:512], lhsT=zT[:, j, tt * 128:(tt + 1) * 128],
                                                               rhs=Wo[:, j, hf * 512:(hf + 1) * 512], start=(j == 0), stop=(j == 7)),
                     r=[('zT', j), ('Wo', j)], w=[kY[hf]])
            fs = slice(hf * 512, (hf + 1) * 512)
            S.op('dve', lambda e, hf=hf, fs=fs: e.tensor_tensor(out=yt[:, fs], in0=psY[hf][:, 0:512], in1=G1B[:, fs], op=ALU.mult),
                 r=[kY[hf], ('G1B', hf)], w=['yt'])
        S.op('dve', lambda e, X=X: e.tensor_tensor(out=X[:], in0=yt[:], in1=X[:], op=ALU.add), r=['yt', kx], w=[kx])
        S.dma('sp', lambda e, tt=tt, X=X: e.dma_start(out=d['y'][tt * 128:(tt + 1) * 128, :], in_=X[:]), 'y%d' % xb, r=[kx], w=[('y', tt)])
    return [('y', tt) for tt in range(NTOK // 128)]


def load_masks(P, d):
    S = P.S
    mk = P.sb("mk", [128, 16])
    S.dma('sp', lambda e: e.dma_start(out=mk[:], in_=d['masks']), 'mk', w=['mk'])
    omk = P.sb("omk", [128, 16])
    S.op('dve', lambda e: e.tensor_scalar(out=omk[:], in0=mk[:], scalar1=-1.0, scalar2=1.0, op0=ALU.mult, op1=ALU.add),
         r=['mk'], w=['omk'])
    return mk, omk


CH = 64
NCH = NTOK // CH


def build_hgrn(pass2):
    P = Prog()
    nc, S = P.nc, P.S
    d = {}
    d['x'] = P.din("x", [NTOK, 1024])
    d['cvec'] = P.din("cvec", [128, 8])
    d['wada'] = P.din("wada", [1024, 3072])
    d['bada'] = P.din("bada", [128, 24])
    d['ng'] = P.din("ng", [128, 8])
    d['win'] = P.din("win", [1024, 5120])
    d['lb'] = P.din("lb", [128, 2, 8])
    d['lsel'] = P.din("lsel", [128, 1])
    d['hgng'] = P.din("hgng", [128, 8])
    d['wout'] = P.din("wout", [1024, 1024])
    d['masks'] = P.din("masks", [128, 16])
    d['sall'] = P.din("sall", [8, 2, 8, 128, 128])
    d['dall'] = P.din("dall", [8, 128, 16])
    if pass2:
        d['y'] = P.dout("y", [NTOK, 1024])
    else:
        d['sloc'] = P.dout("sloc", [2, 8, 128, 128])
        d['dloc'] = P.dout("dloc", [128, 16])

    c = make_consts(P)
    psA = P.ps("psA0")
    psA1 = P.ps("psA1")
    psP = [P.ps("psP0"), P.ps("psP1")]
    ps1 = P.ps("ps1")
    ps2 = P.ps("ps2", (128, 1024), BF16)
    ps3 = P.ps("ps3")
    ps4 = P.ps("ps4")

    wbig = P.sb("wbig", [128, 8, 1024])
    zT = wbig[:].bitcast(BF16)
    fr = mixer_front(P, c, d, psA, psA1, wbig)
    hT = fr['hT']
    mk, omk = load_masks(P, d)

    Mfw = P.sb("Mfw", [128, CH])
    Mbw = P.sb("Mbw", [128, CH])
    io2 = P.sb("io2", [128, CH])
    S.op('pool', lambda e: e.iota(io2[:], [[1, CH]], base=0, channel_multiplier=-1, allow_small_or_imprecise_dtypes=True), w=['io2'])
    S.op('dve', lambda e: e.tensor_single_scalar(Mfw[:], io2[:], 0.0, ALU.is_ge), r=['io2'], w=['Mfw'])
    S.op('dve', lambda e: e.tensor_single_scalar(Mbw[:], io2[:], 0.0, ALU.is_le), r=['io2'], w=['Mbw'])
    Mfwi = P.sb("Mfwi", [128, CH], U32)
    Mbwi = P.sb("Mbwi", [128, CH], U32)
    S.op('dve', lambda e: e.tensor_copy(out=Mfwi[:], in_=Mfw[:]), r=['Mfw'], w=['Mfwi'])
    S.op('dve', lambda e: e.tensor_copy(out=Mbwi[:], in_=Mbw[:]), r=['Mbw'], w=['Mbwi'])

    lbt = P.sb("lbt", [128, 2, 8])
    lbv = P.sb("lbv", [128, 8])
    oml = P.sb("oml", [128, 8])
    lsel = P.sb("lsel", [128, 1])
    hgng = P.sb("hgng", [128, 8])
    S.dma('sp', lambda e: e.dma_start(out=lbt[:], in_=d['lb']), 'lbt', w=['lbt'])
    S.dma('sp', lambda e: e.dma_start(out=lsel[:], in_=d['lsel']), 'lsel', w=['lsel'])
    S.dma('sp', lambda e: e.dma_start(out=hgng[:], in_=d['hgng']), 'hgng', w=['hgng'])
    S.op('dve', lambda e: e.tensor_tensor(out=lbv[:], in0=lbt[:, 1, :], in1=lbt[:, 0, :], op=ALU.subtract), r=['lbt'], w=['lbv'])
    S.op('act', lambda e: e.activation(out=lbv[:], in_=lbv[:], func=AF.Sigmoid), r=['lbv'], w=['lbv'])
    S.op('dve', lambda e: e.tensor_scalar(out=lbv[:], in0=lbv[:], scalar1=lsel[:, 0:1], scalar2=None, op0=ALU.mult),
         r=['lbv', 'lsel'], w=['lbv'])
    S.op('dve', lambda e: e.tensor_scalar(out=oml[:], in0=lbv[:], scalar1=-1.0, scalar2=1.0, op0=ALU.mult, op1=ALU.add),
         r=['lbv'], w=['oml'])

    Wo = None
    if pass2:
        Wo = P.sb("Wo", [128, 8, 1024], BF16)
        for j in range(8):
            S.dma('pool', lambda e, j=j: e.dma_start(out=Wo[:, j, :], in_=d['wout'][j * 128:(j + 1) * 128, :]), 'Wo%d' % j, w=[('Wo', j)])

    Wh = P.sb("Wh", [128, 8, 5, 128], BF16)
    qf = P.sb("qf", [128, NTOK], BF16)
    kf = P.sb("kf", [128, NTOK])
    ff = [P.sb("ff%d" % i, [128, NTOK]) for i in range(2)]
    sg = P.sb("sg", [128, NTOK], BF16)
    Pp = P.sb("Pp", [128, NTOK + 1])
    NPp = P.sb("NPp", [128, NTOK + 1])
    vtok = P.sb("vtok", [CH, NCH, 128], BF16)
    O = P.sb("O", [128, NTOK])
    onesr = c['ones'][:, 0:1].to_broadcast([128, NTOK])
    e123 = P.sb("e123", [128, 3, NCH])
    St = P.sb("St", [128, 128])
    Sp = P.sb("Sp", [128, 128], BF16)
    tmpS = P.sb("tmpS", [128, 128])
    sslot = P.sb("sslot", [128, 8, 128])
    dall = P.sb("dallt", [128, 8, 16])
    alph = P.sb("alph", [128, 8, 16])
    dloc = P.sb("dloc_t", [128, 16])
    eq = [P.sb("eq%d" % i, [128, CH]) for i in range(2)]
    ek = [P.sb("ek%d" % i, [128, CH]) for i in range(2)]
    Qt = [P.sb("Qt%d" % i, [128, CH], BF16) for i in range(2)]
    Kt = [P.sb("Kt%d" % i, [128, CH], BF16) for i in range(2)]
    Asb4 = [P.sb("Asb%d" % i, [CH, CH], BF16) for i in range(4)]
    for i in range(4):
        S.op('dve', lambda e, i=i: e.memset(Asb4[i][:], 0.0), w=[('Asb', i)])
    Ktok = [P.sb("Ktok%d" % i, [CH, 128], BF16) for i in range(2)]
    rstd = P.sb("rstd", [128, 512])
    zt1 = P.sb("zt1", [128, 512])

    if pass2:
        S.dma('sp', lambda e: e.dma_start(out=dall[:], in_=d['dall'].rearrange("s p k -> p s k")), 'dall', w=['dall'])
        for dr in range(2):
            for sl in range(8):
                mi = dr * 8 + sl
                S.op('dve', lambda e, dr=dr, sl=sl, mi=mi: e.tensor_scalar(
                    out=alph[:, sl, dr * 8:(dr + 1) * 8], in0=dall[:, sl, dr * 8:(dr + 1) * 8],
                    scalar1=mk[:, mi:mi + 1], scalar2=omk[:, mi:mi + 1], op0=ALU.mult, op1=ALU.add),
                    r=['dall', 'mk', 'omk'], w=['alph'])

    cnt = dict(u=0)
    for hd in range(8):
        for blk in range(5):
            S.dma('pool', lambda e, blk=blk, hd=hd: e.dma_start(
                out=Wh[:, :, blk, :],
                in_=d['win'][:, blk * 1024 + hd * 128: blk * 1024 + (hd + 1) * 128].rearrange("(kc p) n -> p kc n", p=128)),
                'Wh%d' % blk, w=[('Wh', blk)])
        pcount = 0
        for blk in ([0, 1, 2, 4] if pass2 else [1, 2]):
            for tb in range(4):
                pp = psP[pcount % 2]
                kpp = ('psP', pcount % 2)
                pcount += 1
                for kc in range(8):
                    S.op('pe', lambda e, blk=blk, tb=tb, kc=kc, pp=pp: e.matmul(pp[:, 0:512], lhsT=Wh[:, kc, blk, :],
                                                                               rhs=hT[:, kc, tb * 512:(tb + 1) * 512],
                                                                               start=(kc == 0), stop=(kc == 7)),
                         r=[('Wh', blk), ('hT', kc, tb)], w=[kpp])
                ts_ = slice(tb * 512, (tb + 1) * 512)
                if blk == 0:
                    S.op('act', lambda e, pp=pp, ts_=ts_: e.copy(out=qf[:, ts_], in_=pp[:, 0:512]), r=[kpp], w=[('qf', tb)])
                elif blk == 4:
                    S.op('act', lambda e, pp=pp, ts_=ts_: e.activation(out=sg[:, ts_], in_=pp[:, 0:512], func=AF.Silu), r=[kpp], w=[('sg', tb)])
                else:
                    F = ff[blk - 1]
                    kF = ('ff', blk - 1, tb)
                    S.op('act', lambda e, pp=pp, ts_=ts_, F=F: e.activation(out=F[:, ts_], in_=pp[:, 0:512], func=AF.Sigmoid), r=[kpp], w=[kF])
                    S.op('dve', lambda e, ts_=ts_, F=F, hd=hd: e.tensor_scalar(out=F[:, ts_], in0=F[:, ts_], scalar1=oml[:, hd:hd + 1],
                                                                                scalar2=lbv[:, hd:hd + 1], op0=ALU.mult, op1=ALU.add),
                         r=[kF, 'oml', 'lbv'], w=[kF])
        for cg in range(NCH // 4):
            pp = psP[pcount % 2]
            kpp = ('psP', pcount % 2)
            pcount += 1
            for q in range(4):
                ch = cg * 4 + q
                for kc in range(8):
                    S.op('pe', lambda e, q=q, ch=ch, kc=kc, pp=pp: e.matmul(pp[0:CH, q * 128:(q + 1) * 128],
                                                                           lhsT=hT[:, kc, ch * CH:(ch + 1) * CH], rhs=Wh[:, kc, 3, :],
                                                                           start=(kc == 0), stop=(kc == 7)),
                         r=[('Wh', 3), ('hT', kc, ch // 8)], w=[kpp])
            S.op('dve', lambda e, cg=cg, pp=pp: e.tensor_copy(out=vtok[:, cg * 4:(cg + 1) * 4, :], in_=pp[0:CH, 0:512]),
                 r=[kpp], w=[('vtok', cg // 2)])
        for dr in range(2):
            F = ff[dr]
            allF = [('ff', dr, tb) for tb in range(4)]
            S.op('pool', lambda e, F=F: e.tensor_scalar(out=kf[:], in0=F[:], scalar1=-1.0, scalar2=1.0, op0=ALU.mult, op1=ALU.add),
                 r=allF, w=['kf'])
            S.op('act', lambda e, F=F: e.activation(out=F[:], in_=F[:], func=AF.Ln), r=allF + ['kf'], w=allF)
            S.op('dve', lambda e: e.memset(Pp[:, 0:1], 0.0), w=['Pp0'])
            S.op('dve', lambda e, F=F: e.tensor_tensor_scan(out=Pp[:, 1:NTOK + 1], data0=onesr, data1=F[:], initial=0.0,
                                                             op0=ALU.mult, op1=ALU.add), r=allF + ['c_ones'], w=['Pp'])
            S.op('pool', lambda e: e.tensor_scalar(out=NPp[:], in0=Pp[:], scalar1=-1.0, scalar2=None, op0=ALU.mult),
                 r=['Pp', 'Pp0'], w=['NPp'])
            if dr == 0:
                mid_col = lambda ch: ch * CH + 33
                p_bnd = Pp[:, 0:NTOK:CH]
                p_mid = Pp[:, 33:NTOK:CH]
                p_last = Pp[:, CH:NTOK + 1:CH]
                pairs = [(p_mid, p_bnd), (p_last, p_bnd), (p_last, p_mid)]
            else:
                mid_col = lambda ch: ch * CH + 32
                p_hi = Pp[:, CH:NTOK + 1:CH]
                p_mid = Pp[:, 32:NTOK:CH]
                p_lo = Pp[:, 0:NTOK:CH]
                pairs = [(p_hi, p_mid), (p_hi, p_lo), (p_mid, p_lo)]
            for i, (pa, pb) in enumerate(pairs):
                S.op('dve', lambda e, i=i, pa=pa, pb=pb: e.tensor_tensor(out=e123[:, i, :], in0=pa, in1=pb, op=ALU.subtract),
                     r=['Pp', 'Pp0'], w=['e123'])
            S.op('act', lambda e: e.activation(out=e123[:], in_=e123[:], func=AF.Exp), r=['e123'], w=['e123'])
            di = dr * 8 + hd
            if pass2:
                S.dma('sp', lambda e, dr=dr, hd=hd: e.dma_start(out=sslot[:], in_=d['sall'][:, dr, hd, :, :].rearrange("s p v -> p s v")),
                      'sslot', w=['sslot'])
                S.op('dve', lambda e: e.memset(St[:], 0.0), w=['St'])
                order = list(range(8)) if dr == 0 else list(range(7, -1, -1))
                for sl in order:
                    mi = dr * 8 + sl
                    S.op('act', lambda e, sl=sl, mi=mi: e.activation(out=tmpS[:], in_=sslot[:, sl, :], func=AF.Copy, scale=mk[:, mi:mi + 1]),
                         r=['sslot', 'mk'], w=['tmpS'])
                    S.op('dve', lambda e, sl=sl, di=di: e.scalar_tensor_tensor(out=St[:], in0=St[:], scalar=alph[:, sl, di:di + 1], in1=tmpS[:],
                                                                                op0=ALU.mult, op1=ALU.add),
                         r=['St', 'alph', 'tmpS'], w=['St'])
            else:
                S.op('dve', lambda e: e.memset(St[:], 0.0), w=['St'])
            chunks = list(range(NCH)) if dr == 0 else list(range(NCH - 1, -1, -1))
            Msk = Mfwi if dr == 0 else Mbwi
            kM = 'Mfwi' if dr == 0 else 'Mbwi'
            Asb = Asb4[dr * 2:dr * 2 + 2]
            for ch in chunks:
                u = cnt['u'] % 2
                cnt['u'] += 1
                mc = mid_col(ch)
                if dr == 0:
                    src = Pp[:, ch * CH + 1: ch * CH + 1 + CH]
                    q_scale, q_bias, k_scale, k_bias = 1.0, NPp[:, mc:mc + 1], -1.0, Pp[:, mc:mc + 1]
                else:
                    src = Pp[:, ch * CH: ch * CH + CH]
                    q_scale, q_bias, k_scale, k_bias = -1.0, Pp[:, mc:mc + 1], 1.0, NPp[:, mc:mc + 1]
                cs = slice(ch * CH, (ch + 1) * CH)
                tb = ch // 8
                S.op('act', lambda e, u=u, src=src, k_scale=k_scale, k_bias=k_bias: e.activation(out=ek[u][:], in_=src, func=AF.Exp,
                                                                                             scale=k_scale, bias=k_bias),
                     r=['Pp', 'NPp', 'Pp0'], w=[('ek', u)])
                S.op('pool', lambda e, u=u, cs=cs: e.tensor_tensor(out=Kt[u][:], in0=kf[:, cs], in1=ek[u][:], op=ALU.mult),
                     r=['kf', ('ek', u)], w=[('Kt', u)])
                if pass2:
                    S.op('act', lambda e, u=u, src=src, q_scale=q_scale, q_bias=q_bias: e.activation(out=eq[u][:], in_=src, func=AF.Exp,
                                                                                                 scale=q_scale, bias=q_bias),
                         r=['Pp', 'NPp', 'Pp0'], w=[('eq', u)])
                    S.op('dve', lambda e, u=u, cs=cs: e.tensor_tensor(out=Qt[u][:], in0=qf[:, cs], in1=eq[u][:], op=ALU.mult),
                         r=[('qf', tb), ('eq', u)], w=[('Qt', u)])
                    S.op('pe', lambda e, u=u: e.matmul(ps1[0:CH, 0:CH], lhsT=Kt[u][:], rhs=Qt[u][:], start=True, stop=True),
                         r=[('Kt', u), ('Qt', u)], w=['ps1'])
                    S.op('dve', lambda e, u=u, Msk=Msk, Asb=Asb: e.copy_predicated(out=Asb[u][:], mask=Msk[0:CH, :], data=ps1[0:CH, 0:CH]),
                         r=['ps1', kM], w=[('Asb', dr * 2 + u)])
                S.op('pe', lambda e, u=u: e.transpose(out=ps2[0:CH, 0:128], in_=Kt[u][:], identity=c['identb'][:]),
                     r=[('Kt', u), 'c_identb'], w=['ps2'])
                S.op('act', lambda e, u=u: e.copy(out=Ktok[u][:], in_=ps2[0:CH, 0:128]), r=['ps2'], w=[('Ktok', u)])
                if pass2:
                    S.op('dve', lambda e, ch=ch: e.tensor_scalar(out=Sp[:], in0=St[:], scalar1=e123[:, 0, ch:ch + 1], scalar2=None, op0=ALU.mult),
                         r=['St', 'e123'], w=['Sp'])
                    S.op('pe', lambda e, u=u, ch=ch, Asb=Asb: e.matmul(ps3[:, 0:CH], lhsT=vtok[:, ch, :], rhs=Asb[u][:], start=True, stop=False),
                         r=[('vtok', ch // 8), ('Asb', dr * 2 + u)], w=['ps3'])
                    S.op('pe', lambda e, u=u: e.matmul(ps3[:, 0:CH], lhsT=Sp[:], rhs=Qt[u][:], start=False, stop=True),
                         r=['Sp', ('Qt', u)], w=['ps3'])
                    if dr == 0:
                        S.op('act', lambda e, cs=cs: e.copy(out=O[:, cs], in_=ps3[:, 0:CH]), r=['ps3'], w=[('O', tb)])
                    else:
                        S.op('dve', lambda e, cs=cs: e.tensor_tensor(out=O[:, cs], in0=ps3[:, 0:CH], in1=O[:, cs], op=ALU.add),
                             r=['ps3', ('O', tb)], w=[('O', tb)])
                S.op('pe', lambda e, u=u, ch=ch: e.matmul(ps4[:, 0:128], lhsT=Ktok[u][:], rhs=vtok[:, ch, :], start=True, stop=True),
                     r=[('Ktok', u), ('vtok', ch // 8)], w=['ps4'])
                S.op('act', lambda e, ch=ch: e.activation(out=tmpS[:], in_=ps4[:, 0:128], func=AF.Copy, scale=e123[:, 2, ch:ch + 1]),
                     r=['ps4', 'e123'], w=['tmpS'])
                S.op('dve', lambda e, ch=ch: e.scalar_tensor_tensor(out=St[:], in0=St[:], scalar=e123[:, 1, ch:ch + 1], in1=tmpS[:],
                                                                    op0=ALU.mult, op1=ALU.add),
                     r=['St', 'e123', 'tmpS', 'Sp'], w=['St'])
            if not pass2:
                S.dma('sp', lambda e, dr=dr, hd=hd: e.dma_start(out=d['sloc'][dr, hd], in_=St[:]), 'sloc', r=['St'], w=[('sloc', di)])
                S.op('act', lambda e, di=di: e.activation(out=dloc[:, di:di + 1], in_=Pp[:, NTOK:NTOK + 1], func=AF.Exp),
                     r=['Pp'], w=[('dloc', di)])
        if pass2:
            allO = [('O', tb) for tb in range(4)]
            S.op('act', lambda e: e.activation(out=kf[:], in_=O[:], func=AF.Square), r=allO, w=['kf'])
            for tb in range(4):
                pp = psP[pcount % 2]
                kpp = ('psP', pcount % 2)
                pcount += 1
                ts_ = slice(tb * 512, (tb + 1) * 512)
                S.op('pe', lambda e, pp=pp, ts_=ts_: e.matmul(pp[:, 0:512], lhsT=c['ones'][:], rhs=kf[:, ts_], start=True, stop=True),
                     r=['kf', 'c_ones'], w=[kpp])
                S.op('act', lambda e, pp=pp: e.activation(out=rstd[:], in_=pp[:, 0:512], func=AF.Ln, scale=1.0 / 128, bias=c['eps'][:]),
                     r=[kpp, 'c_eps'], w=['rstd'])
                S.op('act', lambda e: e.activation(out=rstd[:], in_=rstd[:], func=AF.Exp, scale=-0.5), r=['rstd'], w=['rstd'])
                S.op('dve', lambda e, ts_=ts_, hd=hd: e.scalar_tensor_tensor(out=zt1[:], in0=O[:, ts_], scalar=hgng[:, hd:hd + 1], in1=rstd[:],
                                                                             op0=ALU.mult, op1=ALU.mult),
                     r=allO + ['hgng', 'rstd'], w=['zt1'])
                S.op('dve', lambda e, ts_=ts_, hd=hd: e.tensor_tensor(out=zT[:, hd, ts_], in0=zt1[:], in1=sg[:, ts_], op=ALU.mult),
                     r=['zt1', ('sg', tb), ('wbig', 0), ('wbig', 1)], w=[('zT', hd)])
    if pass2:
        outk = mixer_back(P, c, d, fr, zT, Wo, psP, [('psP', 0), ('psP', 1)])
        S.fence('sp', r=outk)
    else:
        S.dma('sp', lambda e: e.dma_start(out=d['dloc'], in_=dloc[:]), 'dlocd', r=[('dloc', i) for i in range(16)], w=['dlocd'])
        S.fence('sp', r=[('sloc', i) for i in range(16)] + ['dlocd'])
    S.emit()
    P.st.close()
    return nc


def build_lru(pass2):
    P = Prog()
    nc, S = P.nc, P.S
    d = {}
    d['x'] = P.din("x", [NTOK, 1024])
    d['xhalo'] = P.din("xhalo", [3, 1024])
    d['hmask'] = P.din("hmask", [128, 2])
    d['cvec'] = P.din("cvec", [128, 8])
    d['wada'] = P.din("wada", [1024, 3072])
    d['bada'] = P.din("bada", [128, 24])
    d['ng'] = P.din("ng", [128, 8])
    d['win'] = P.din("win", [1024, 2048])
    d['convw'] = P.din("convw", [128, 4, 8])
    d['convb'] = P.din("convb", [128, 8])
    d['wa'] = P.din("wa", [2, 4, 256, 256])
    d['wx'] = P.din("wx", [2, 4, 256, 256])
    d['ba'] = P.din("ba", [128, 2, 8])
    d['bx'] = P.din("bx", [128, 2, 8])
    d['lam'] = P.din("lam", [128, 2, 8])
    d['wout'] = P.din("wout", [1024, 1024])
    d['masks'] = P.din("masks", [128, 16])
    d['hall'] = P.din("hall", [8, 128, 16])
    d['aall'] = P.din("aall", [8, 128, 16])
    if pass2:
        d['y'] = P.dout("y", [NTOK, 1024])
    else:
        d['hloc'] = P.dout("hloc", [128, 16])
        d['aloc'] = P.dout("aloc", [128, 16])

    c = make_consts(P)
    psA = P.ps("psA0")
    psA1 = P.ps("psA1")
    psP = [P.ps("psP0"), P.ps("psP1")]
    psR = [P.ps("psR0"), P.ps("psR1")]
    psI = [P.ps("psI0"), P.ps("psI1")]

    wbig = P.sb("wbig", [128, 8, 1024])
    zT = wbig[:].bitcast(BF16)
    fr = mixer_front(P, c, d, psA, psA1, wbig, need_halo=True)
    hT, hTh = fr['hT'], fr['hTh']
    mk, omk = load_masks(P, d)

    def small_in(name, shape):
        t = P.sb(name + "_t", shape)
        S.dma('sp', lambda e: e.dma_start(out=t[:], in_=d[name]), name, w=[name])
        return t
    hmask = small_in('hmask', [128, 2])
    cw = small_in('convw', [128, 4, 8])
    cb = small_in('convb', [128, 8])
    ba = small_in('ba', [128, 2, 8])
    bx = small_in('bx', [128, 2, 8])
    lam = small_in('lam', [128, 2, 8])
    ncl = P.sb("ncl", [128, 2, 8])
    ncl2 = P.sb("ncl2", [128, 2, 8])
    S.op('act', lambda e: e.activation(out=ncl[:], in_=lam[:], func=AF.Exp, scale=-1.0), r=['lam'], w=['ncl'])
    S.op('act', lambda e: e.activation(out=ncl[:], in_=ncl[:], func=AF.Ln, bias=c['ones'][:, 0:1]), r=['ncl', 'c_ones'], w=['ncl'])
    S.op('dve', lambda e: e.tensor_scalar(out=ncl2[:], in0=ncl[:], scalar1=-16.0, scalar2=None, op0=ALU.mult), r=['ncl'], w=['ncl2'])
    S.op('dve', lambda e: e.tensor_scalar(out=ncl[:], in0=ncl[:], scalar1=-8.0, scalar2=None, op0=ALU.mult), r=['ncl', 'ncl2'], w=['ncl'])

    hin = P.sb("hin", [128, 16])
    if pass2:
        hall = P.sb("hall_t", [128, 8, 16])
        aall = P.sb("aall_t", [128, 8, 16])
        tmph = P.sb("tmph", [128, 8])
        S.dma('sp', lambda e: e.dma_start(out=hall[:], in_=d['hall'].rearrange("s p k -> p s k")), 'hall', w=['hall'])
        S.dma('sp', lambda e: e.dma_start(out=aall[:], in_=d['aall'].rearrange("s p k -> p s k")), 'aall', w=['aall'])
        S.op('dve', lambda e: e.memset(hin[:], 0.0), w=['hin'])
        for dr in range(2):
            order = list(range(8)) if dr == 0 else list(range(7, -1, -1))
            ds = slice(dr * 8, (dr + 1) * 8)
            for sl in order:
                mi = dr * 8 + sl
                S.op('dve', lambda e, sl=sl, ds=ds, mi=mi: e.tensor_scalar(out=aall[:, sl, ds], in0=aall[:, sl, ds], scalar1=mk[:, mi:mi + 1],
                                                                          scalar2=omk[:, mi:mi + 1], op0=ALU.mult, op1=ALU.add),
                     r=['aall', 'mk', 'omk'], w=['aall'])
                S.op('dve', lambda e, sl=sl, ds=ds, mi=mi: e.tensor_scalar(out=tmph[:], in0=hall[:, sl, ds], scalar1=mk[:, mi:mi + 1], scalar2=None,
                                                                          op0=ALU.mult), r=['hall', 'mk'], w=['tmph'])
                S.op('dve', lambda e, sl=sl, ds=ds: e.tensor_tensor(out=hin[:, ds], in0=hin[:, ds], in1=aall[:, sl, ds], op=ALU.mult),
                     r=['hin', 'aall'], w=['hin'])
                S.op('dve', lambda e, ds=ds: e.tensor_tensor(out=hin[:, ds], in0=hin[:, ds], in1=tmph[:], op=ALU.add),
                     r=['hin', 'tmph'], w=['hin'])

    Wo = None
    if pass2:
        Wo = P.sb("Wo", [128, 8, 1024], BF16)
        for j in range(8):
            S.dma('pool', lambda e, j=j: e.dma_start(out=Wo[:, j, :], in_=d['wout'][j * 128:(j + 1) * 128, :]), 'Wo%d' % j, w=[('Wo', j)])

    Wl = P.sb("Wl", [128, 8, 4, 128], BF16)
    Wg = P.sb("Wg", [128, 2, 2, 2, 256], BF16)
    xb = P.sb("xb", [128, 2, NTOK + 4])
    xc = P.sb("xc", [128, 2, NTOK])
    xcb = P.sb("xcb", [128, 2, NTOK], BF16)
    yg = P.sb("yg", [128, 2, NTOK], BF16)
    a_arr = P.sb("a_arr", [128, NTOK])
    bt_arr = P.sb("bt_arr", [128, NTOK])
    hsum = P.sb("hsum", [128, 2, NTOK])
    rt = P.sb("rt", [128, 512])
    igt = P.sb("igt", [128, 512])
    a2t = P.sb("a2t", [128, 512])
    rs = P.sb("rs", [128, 8])
    hloc = P.sb("hloc_t", [128, 16])
    aloc = P.sb("aloc_t", [128, 16])
    psH = psA

    for blk in range(4):
        for ti in range(4):
            col0 = (ti // 2) * 1024 + blk * 256 + (ti % 2) * 128
            S.dma('pool', lambda e, ti=ti, col0=col0: e.dma_start(
                out=Wl[:, :, ti, :], in_=d['win'][:, col0:col0 + 128].rearrange("(kc p) n -> p kc n", p=128)),
                'Wl%d' % ti, w=[('Wl', ti)])
        for dr in range(2):
            for gi, nm in enumerate(['wa', 'wx']):
                S.dma('pool', lambda e, dr=dr, gi=gi, nm=nm, blk=blk: e.dma_start(
                    out=Wg[:, dr, gi, :, :], in_=d[nm][dr, blk].rearrange("(kc p) n -> p kc n", p=128)),
                    'Wg%d%d' % (dr, gi), w=[('Wg', dr, gi)])
        pcount = 0
        for ti in range(4):
            ct = ti % 2
            for tb in range(4):
                pp = psP[pcount % 2]
                kpp = ('psP', pcount % 2)
                pcount += 1
                for kc in range(8):
                    S.op('pe', lambda e, ti=ti, tb=tb, kc=kc, pp=pp: e.matmul(pp[:, 0:512], lhsT=Wl[:, kc, ti, :],
                                                                            rhs=hT[:, kc, tb * 512:(tb + 1) * 512],
                                                                            start=(kc == 0), stop=(kc == 7)),
                         r=[('Wl', ti), ('hT', kc, tb)], w=[kpp])
                if ti < 2:
                    S.op('act', lambda e, ct=ct, tb=tb, pp=pp: e.copy(out=xb[:, ct, 2 + tb * 512: 2 + (tb + 1) * 512], in_=pp[:, 0:512]),
                         r=[kpp], w=[('xb', ct, tb)])
                else:
                    S.op('act', lambda e, ct=ct, tb=tb, pp=pp: e.activation(out=yg[:, ct, tb * 512:(tb + 1) * 512], in_=pp[:, 0:512],
                                                                            func=AF.Gelu_apprx_tanh), r=[kpp], w=[('yg', ct)])
        for ct in range(2):
            for kc in range(8):
                S.op('pe', lambda e, ct=ct, kc=kc: e.matmul(psH[:, ct * 4: ct * 4 + 3], lhsT=Wl[:, kc, ct, :], rhs=hTh[:, kc, 0:3],
                                                           start=(kc == 0), stop=(kc == 7)),
                     r=[('Wl', ct), 'hTh'], w=['psA'])
            S.op('dve', lambda e, ct=ct: e.tensor_scalar(out=xb[:, ct, 0:2], in0=psH[:, ct * 4: ct * 4 + 2], scalar1=hmask[:, 0:1], scalar2=None,
                                                        op0=ALU.mult), r=['psA', 'hmask'], w=[('xb', ct, 'hl')])
            S.op('dve', lambda e, ct=ct: e.tensor_scalar(out=xb[:, ct, NTOK + 2:NTOK + 3], in0=psH[:, ct * 4 + 2: ct * 4 + 3], scalar1=hmask[:, 1:2],
                                                        scalar2=None, op0=ALU.mult), r=['psA', 'hmask'], w=[('xb', ct, 'hr')])
        for ct in range(2):
            cc = blk * 2 + ct
            allxb = [('xb', ct, tb) for tb in range(4)] + [('xb', ct, 'hl'), ('xb', ct, 'hr')]
            S.op('dve', lambda e, ct=ct, cc=cc: e.tensor_scalar(out=xc[:, ct, :], in0=xb[:, ct, 0:NTOK], scalar1=cw[:, 0, cc:cc + 1],
                                                               scalar2=cb[:, cc:cc + 1], op0=ALU.mult, op1=ALU.add),
                 r=allxb + ['convw', 'convb'], w=[('xc', ct)])
            for j in range(1, 4):
                S.op('dve', lambda e, ct=ct, cc=cc, j=j: e.scalar_tensor_tensor(out=xc[:, ct, :], in0=xb[:, ct, j:j + NTOK], scalar=cw[:, j, cc:cc + 1],
                                                                               in1=xc[:, ct, :], op0=ALU.mult, op1=ALU.add),
                     r=allxb + ['convw', ('xc', ct)], w=[('xc', ct)])
            S.op('act', lambda e, ct=ct: e.copy(out=xcb[:, ct, :], in_=xc[:, ct, :]), r=[('xc', ct)], w=[('xcb', ct)])
        for dr in range(2):
            for ct in range(2):
                cc = blk * 2 + ct
                di = dr * 8 + cc
                for tb in range(4):
                    ts_ = slice(tb * 512, (tb + 1) * 512)
                    u = tb % 2
                    for gi, pg, kg in ((0, psR[u], ('psR', u)), (1, psI[u], ('psI', u))):
                        for kc in range(2):
                            S.op('pe', lambda e, dr=dr, gi=gi, kc=kc, ct=ct, ts_=ts_, pg=pg: e.matmul(
                                pg[:, 0:512], lhsT=Wg[:, dr, gi, kc, ct * 128:(ct + 1) * 128], rhs=xcb[:, kc, ts_],
                                start=(kc == 0), stop=(kc == 1)), r=[('Wg', dr, gi), ('xcb', kc)], w=[kg])
                    S.op('act', lambda e, dr=dr, cc=cc, u=u, tb=tb: e.activation(out=rt[:], in_=psR[u][:, 0:512], func=AF.Sigmoid,
                                                                                bias=ba[:, dr, cc:cc + 1], accum_out=rs[:, tb:tb + 1]),
                         r=[('psR', u), 'ba'], w=['rt', ('rs', tb)])
                    S.op('act', lambda e, dr=dr, cc=cc, u=u: e.activation(out=igt[:], in_=psI[u][:, 0:512], func=AF.Sigmoid,
                                                                         bias=bx[:, dr, cc:cc + 1]), r=[('psI', u), 'bx'], w=['igt'])
                    S.op('act', lambda e, dr=dr, cc=cc, ts_=ts_: e.activation(out=a_arr[:, ts_], in_=rt[:], func=AF.Exp,
                                                                             scale=ncl[:, dr, cc:cc + 1]), r=['rt', 'ncl'], w=[('a_arr', tb)])
                    S.op('act', lambda e, dr=dr, cc=cc: e.activation(out=a2t[:], in_=rt[:], func=AF.Exp, scale=ncl2[:, dr, cc:cc + 1]),
                         r=['rt', 'ncl2'], w=['a2t'])
                    S.op('dve', lambda e: e.tensor_scalar(out=a2t[:], in0=a2t[:], scalar1=-1.0, scalar2=1.0, op0=ALU.mult, op1=ALU.add),
                         r=['a2t'], w=['a2t'])
                    S.op('act', lambda e: e.activation(out=a2t[:], in_=a2t[:], func=AF.Sqrt), r=['a2t'], w=['a2t'])
                    S.op('dve', lambda e: e.tensor_tensor(out=igt[:], in0=igt[:], in1=a2t[:], op=ALU.mult), r=['igt', 'a2t'], w=['igt'])
                    S.op('dve', lambda e, ct=ct, ts_=ts_: e.tensor_tensor(out=bt_arr[:, ts_], in0=igt[:], in1=xc[:, ct, ts_], op=ALU.mult),
                         r=['igt', ('xc', ct)], w=[('bt_arr', tb)])
                alla = [('a_arr', tb) for tb in range(4)]
                allb = [('bt_arr', tb) for tb in range(4)]
                init = hin[:, di:di + 1] if pass2 else 0.0
                rinit = ['hin'] if pass2 else []
                if dr == 0:
                    dst = hsum[:, ct, :] if pass2 else bt_arr[:]
                    kdst = [('hsum', ct)] if pass2 else allb
                    S.op('dve', lambda e, dst=dst, init=init: e.tensor_tensor_scan(out=dst, data0=a_arr[:], data1=bt_arr[:], initial=init,
                                                                                  op0=ALU.mult, op1=ALU.add),
                         r=alla + allb + rinit, w=kdst)
                else:
                    S.op('dve', lambda e, init=init: e.tensor_tensor_scan(out=bt_arr[:, ::-1], data0=a_arr[:, ::-1], data1=bt_arr[:, ::-1],
                                                                         initial=init, op0=ALU.mult, op1=ALU.add),
                         r=alla + allb + rinit, w=allb)
                    if pass2:
                        S.op('dve', lambda e, ct=ct: e.tensor_tensor(out=hsum[:, ct, :], in0=hsum[:, ct, :], in1=bt_arr[:], op=ALU.add),
                             r=allb + [('hsum', ct)], w=[('hsum', ct)])
                if not pass2:
                    lastc = NTOK - 1 if dr == 0 else 0
                    S.op('dve', lambda e, di=di, lastc=lastc: e.tensor_copy(out=hloc[:, di:di + 1], in_=bt_arr[:, lastc:lastc + 1]),
                         r=allb, w=[('hloc', di)])
                    S.op('dve', lambda e, di=di: e.tensor_reduce(out=aloc[:, di:di + 1], in_=rs[:, 0:4], axis=AX.X, op=ALU.add),
                         r=[('rs', tb) for tb in range(4)], w=[('aloc', di)])
                    S.op('act', lambda e, di=di, dr=dr, cc=cc: e.activation(out=aloc[:, di:di + 1], in_=aloc[:, di:di + 1], func=AF.Exp,
                                                                           scale=ncl[:, dr, cc:cc + 1]), r=[('aloc', di), 'ncl'], w=[('aloc', di)])
        if pass2:
            for ct in range(2):
                cc = blk * 2 + ct
                S.op('dve', lambda e, ct=ct, cc=cc: e.tensor_tensor(out=zT[:, cc, :], in0=hsum[:, ct, :], in1=yg[:, ct, :], op=ALU.mult),
                     r=[('hsum', ct), ('yg', ct), ('wbig', 0), ('wbig', 1)], w=[('zT', cc)])
    if pass2:
        outk = mixer_back(P, c, d, fr, zT, Wo, psP, [('psP', 0), ('psP', 1)])
        S.fence('sp', r=outk)
    else:
        S.dma('sp', lambda e: e.dma_start(out=d['hloc'], in_=hloc[:]), 'hlocd', r=[('hloc', i) for i in range(16)], w=['hlocd'])
        S.dma('sp', lambda e: e.dma_start(out=d['aloc'], in_=aloc[:]), 'alocd', r=[('aloc', i) for i in range(16)], w=['alocd'])
        S.fence('sp', r=['hlocd', 'alocd'])
    S.emit()
    P.st.close()
    return nc


def lay8(v):
    return np.ascontiguousarray(np.asarray(v, np.float32).reshape(-1, 128).T)


def core_masks(k):
    b, pos = k // 4, k % 4
    m = np.zeros((128, 16), np.float32)
    for j in range(8):
        if j // 4 == b and j % 4 < pos:
            m[:, j] = 1.0
        if j // 4 == b and j % 4 > pos:
            m[:, 8 + j] = 1.0
    return m


def core_x(x, k):
    b, pos = k // 4, k % 4
    return np.ascontiguousarray(x[b, pos * 2048:(pos + 1) * 2048])


def hgrn_inputs(inp, i, xs, k, sall, dall):
    j = i // 2
    b = k // 4
    return {
        "x": xs,
        "cvec": lay8(inp["c"][b]),
        "wada": np.ascontiguousarray(inp["w_ada"][i][:, :3072]),
        "bada": lay8(inp["b_ada"][i][:3072]),
        "ng": lay8(inp["norm_g"][i, 0]),
        "win": np.ascontiguousarray(inp["hg_w_in"][j]),
        "lb": np.ascontiguousarray(np.stack([lay8(inp["hg_lb"][0]), lay8(inp["hg_lb"][1])], axis=1)),
        "lsel": np.full((128, 1), float(j), np.float32),
        "hgng": lay8(inp["hg_norm_g"][j]),
        "wout": np.ascontiguousarray(inp["hg_w_out"][j]),
        "masks": core_masks(k),
        "sall": sall,
        "dall": dall,
    }


def lay_d8(v):
    v = np.asarray(v, np.float32)
    return np.ascontiguousarray(v.reshape(v.shape[0], 8, 128).transpose(2, 0, 1))


def core_halo(x, k):
    b, pos = k // 4, k % 4
    s, e = pos * 2048, (pos + 1) * 2048
    h = np.zeros((3, 1024), np.float32)
    if pos > 0:
        h[0:2] = x[b, s - 2:s]
    if pos < 3:
        h[2] = x[b, e]
    m = np.zeros((128, 2), np.float32)
    m[:, 0] = 1.0 if pos > 0 else 0.0
    m[:, 1] = 1.0 if pos < 3 else 0.0
    return h, m


def lru_inputs(inp, i, xs, xh, hm, k, hall, aall):
    j = i // 2
    b = k // 4
    return {
        "x": xs, "xhalo": xh, "hmask": hm,
        "cvec": lay8(inp["c"][b]),
        "wada": np.ascontiguousarray(inp["w_ada"][i][:, :3072]),
        "bada": lay8(inp["b_ada"][i][:3072]),
        "ng": lay8(inp["norm_g"][i, 0]),
        "win": np.ascontiguousarray(inp["lru_w_in"][j]),
        "convw": lay_d8(inp["lru_conv_w"][j]),
        "convb": lay8(inp["lru_conv_b"][j]),
        "wa": np.ascontiguousarray(inp["lru_w_a"][j]),
        "wx": np.ascontiguousarray(inp["lru_w_x"][j]),
        "ba": lay_d8(inp["lru_b_a"][j]),
        "bx": lay_d8(inp["lru_b_x"][j]),
        "lam": lay_d8(inp["lru_lam"][j]),
        "wout": np.ascontiguousarray(inp["lru_w_out"][j]),
        "masks": core_masks(k),
        "hall": hall, "aall": aall,
    }


_PROGS = {}


def _prog(name):
    if name not in _PROGS:
        if name == "hg1":
            _PROGS[name] = build_hgrn(False)
        elif name == "hg2":
            _PROGS[name] = build_hgrn(True)
        elif name == "lru1":
            _PROGS[name] = build_lru(False)
        elif name == "lru2":
            _PROGS[name] = build_lru(True)
        elif name == "peer":
            _PROGS[name] = build_peer(False)
        elif name == "peerf":
            _PROGS[name] = build_peer(True)
    return _PROGS[name]


def _run(name, in_maps):
    res = run_bass_kernel_spmd(_prog(name), in_maps, core_ids=list(range(8)))
    return res.results


def peer_inputs(inp, i, xs, b):
    return {
        "x": xs,
        "cvec": lay8(inp["c"][b]),
        "wada": np.ascontiguousarray(inp["w_ada"][i][:, 3072:]),
        "bada": lay8(inp["b_ada"][i][3072:]),
        "ng": lay8(inp["norm_g"][i, 1]),
        "wq": np.ascontiguousarray(inp["peer_w_q"][i]),
        "keys": np.ascontiguousarray(inp["peer_keys"][i].reshape(16, 128, 128)),
        "utab": np.ascontiguousarray(inp["peer_u"][i]),
        "vtab": np.ascontiguousarray(inp["peer_v"][i]),
        "fg": lay8(inp["final_g"]),
    }


def _assemble(rs, key="y"):
    return np.stack([np.asarray(rs[k][key]) for k in range(8)]).reshape(2, 8192, 1024)


def kernel(**inputs):
    inp = {k: np.asarray(v, dtype=np.float32) for k, v in inputs.items()}
    x = np.ascontiguousarray(inp["x"])
    depth = inp["w_ada"].shape[0]
    for i in range(depth):
        xs = [core_x(x, k) for k in range(8)]
        if i % 2 == 0:
            s0 = np.zeros((8, 2, 8, 128, 128), np.float32)
            d0 = np.zeros((8, 128, 16), np.float32)
            r1 = _run("hg1", [hgrn_inputs(inp, i, xs[k], k, s0, d0) for k in range(8)])
            sall = np.ascontiguousarray(np.stack([np.asarray(r1[k]["sloc"]) for k in range(8)]))
            dall = np.ascontiguousarray(np.stack([np.asarray(r1[k]["dloc"]) for k in range(8)]))
            r2 = _run("hg2", [hgrn_inputs(inp, i, xs[k], k, sall, dall) for k in range(8)])
        else:
            hh = [core_halo(x, k) for k in range(8)]
            z16 = np.zeros((8, 128, 16), np.float32)
            r1 = _run("lru1", [lru_inputs(inp, i, xs[k], hh[k][0], hh[k][1], k, z16, z16) for k in range(8)])
            hall = np.ascontiguousarray(np.stack([np.asarray(r1[k]["hloc"]) for k in range(8)]))
            aall = np.ascontiguousarray(np.stack([np.asarray(r1[k]["aloc"]) for k in range(8)]))
            r2 = _run("lru2", [lru_inputs(inp, i, xs[k], hh[k][0], hh[k][1], k, hall, aall) for k in range(8)])
        x = _assemble(r2)
        xs = [core_x(x, k) for k in range(8)]
        r3 = _run("peerf" if i == depth - 1 else "peer", [peer_inputs(inp, i, xs[k], k // 4) for k in range(8)])
        x = _assemble(r3)
    return np.ascontiguousarray(x.astype(np.float32))
```
